# Optimizing a Trainium2 kernel written in Bass

```python
import math
import jax
import jax.numpy as jnp
from jax import lax
import numpy as np

D_MODEL = 2048
BATCH = 4
SEQ = 8192
DEPTH = 1

CTX_LEN = 256
GRID_W = 64
SSM_EXPAND = 2
D_INNER = SSM_EXPAND * D_MODEL
SSM_HEAD_DIM = 64
SSM_HEADS = D_INNER // SSM_HEAD_DIM
SSM_GROUPS = 8
HEADS_PER_GROUP = SSM_HEADS // SSM_GROUPS
SSM_STATE = 128
SSM_CONV = 5
CHUNK = 128
CONV_DIM = D_INNER + 2 * SSM_GROUPS * SSM_STATE
D_CF = D_MODEL
CF_KERNEL = 31
D_FF = -(-8 * D_MODEL // (3 * 256)) * 256
EPS = 1e-6
OFF_Z = D_INNER
OFF_XBC = OFF_Z + CONV_DIM
OFF_DT = OFF_XBC + 2 * SSM_HEADS
OFF_GLU = OFF_DT + 2 * D_CF
D_IN_TOTAL = OFF_GLU + 2 * D_MODEL

kernel_name = "hybrid_ssd_conformer_dit_block"


def rmsnorm(x, w):
    xf = x.astype(jnp.float32)
    y = xf * lax.rsqrt(jnp.mean(xf * xf, axis=-1, keepdims=True) + EPS)
    return (y * w.astype(jnp.float32)).astype(x.dtype)


def layernorm(x, g, b):
    xf = x.astype(jnp.float32)
    mu = jnp.mean(xf, axis=-1, keepdims=True)
    xc = xf - mu
    var = jnp.mean(xc * xc, axis=-1, keepdims=True)
    y = xc * lax.rsqrt(var + EPS) * g.astype(jnp.float32) + b.astype(jnp.float32)
    return y.astype(x.dtype)


def ada_params(cvec, w_mod, b_mod):
    m = jax.nn.silu(cvec) @ w_mod + b_mod
    return jnp.split(m, 6, axis=-1)


def modulate(h, shift, scale):
    return h * (1 + scale) + shift


def dwconv_seq(u, w, bias):
    k = w.shape[0]
    out = lax.conv_general_dilated(
        u, w[:, None, :].astype(u.dtype), window_strides=(1,), padding=[(k // 2, k // 2)],
        dimension_numbers=("NWC", "WIO", "NWC"), feature_group_count=u.shape[-1])
    return out + bias.astype(u.dtype)


def dwconv_grid_columns(u, w, bias, rows):
    b, L, C = u.shape
    k = w.shape[0]
    ug = u.reshape(b, rows, GRID_W, C)
    out = lax.conv_general_dilated(
        ug, w[:, None, None, :].astype(u.dtype), window_strides=(1, 1),
        padding=[(k // 2, k // 2), (0, 0)],
        dimension_numbers=("NHWC", "HWIO", "NHWC"), feature_group_count=C)
    return out.reshape(b, L, C) + bias.astype(u.dtype)


def ssd_chunked(xs, bm, cm, dt, a, h0):
    b, L = xs.shape[:2]
    nc = L // CHUNK

    def to_chunks(t):
        return jnp.moveaxis(t.reshape(b, nc, CHUNK, *t.shape[2:]), 1, 0)

    xdt = (xs.astype(jnp.float32) * dt[..., None]).reshape(
        b, L, SSM_GROUPS, HEADS_PER_GROUP, SSM_HEAD_DIM)
    da = (dt * a).reshape(b, L, SSM_GROUPS, HEADS_PER_GROUP)
    lower = jnp.tril(jnp.ones((CHUNK, CHUNK), dtype=bool))[None, :, :, None, None]

    def step(h, inp):
        xdt_c, b_c, c_c, da_c = inp
        cs = jnp.cumsum(da_c, axis=1)
        seg = cs[:, :, None] - cs[:, None, :]
        decay = jnp.exp(jnp.where(lower, seg, -jnp.inf))
        cb = jnp.einsum("blgn,bsgn->blsg", c_c, b_c)
        y_diag = jnp.einsum("blsgj,bsgjp->blgjp", cb[..., None] * decay, xdt_c)
        y_off = jnp.einsum("blgn,bgjpn->blgjp", c_c, h) * jnp.exp(cs)[..., None]
        to_end = jnp.exp(cs[:, -1:] - cs)
        h_new = h * jnp.exp(cs[:, -1])[..., None, None] + jnp.einsum(
            "bsgn,bsgjp->bgjpn", b_c, xdt_c * to_end[..., None])
        return h_new, y_diag + y_off

    h_last, ys = lax.scan(step, h0, (to_chunks(xdt), to_chunks(bm.astype(jnp.float32)),
                                     to_chunks(cm.astype(jnp.float32)), to_chunks(da)))
    y = jnp.moveaxis(ys, 0, 1).reshape(b, L, SSM_HEADS, SSM_HEAD_DIM)
    return y, h_last


def flip_seq(t):
    return jnp.flip(t, axis=1)


def mixer_front(h, w_in, conv_w, conv_b, dt_bias, a_log, h0_f, h0_b):
    b, L, _ = h.shape
    proj = h @ w_in
    z, xbc, dt_raw, glu, gates = jnp.split(proj, [OFF_Z, OFF_XBC, OFF_DT, OFF_GLU], axis=-1)
    xbc = jax.nn.silu(dwconv_seq(xbc, conv_w, conv_b))
    xs, bm, cm = jnp.split(xbc, [D_INNER, D_INNER + SSM_GROUPS * SSM_STATE], axis=-1)
    xs = xs.reshape(b, L, SSM_HEADS, SSM_HEAD_DIM)
    bm = bm.reshape(b, L, SSM_GROUPS, SSM_STATE)
    cm = cm.reshape(b, L, SSM_GROUPS, SSM_STATE)
    dt = jax.nn.softplus(dt_raw.astype(jnp.float32).reshape(b, L, 2, SSM_HEADS)
                         + dt_bias.astype(jnp.float32))
    a = -jnp.exp(a_log.astype(jnp.float32))
    y_f, h_f = ssd_chunked(xs, bm, cm, dt[:, :, 0], a[0], h0_f)
    y_b, h_b = ssd_chunked(flip_seq(xs), flip_seq(bm), flip_seq(cm), flip_seq(dt[:, :, 1]), a[1], h0_b)
    y = y_f + flip_seq(y_b)
    return (z, xs, y, glu, gates), h_f, h_b


def mixer_back(parts, d_skip, ssm_norm, cf_conv, cf_ln_g, cf_ln_b, w_proj_a, w_proj_b, w_out):
    z, xs, y, glu, gates = parts
    b, L = z.shape[:2]
    y = (y + d_skip.astype(jnp.float32)[:, None] * xs.astype(jnp.float32)).reshape(
        b, L, D_INNER).astype(z.dtype)
    y_a = rmsnorm(y * jax.nn.silu(z), ssm_norm) @ w_proj_a
    u, v = jnp.split(glu, 2, axis=-1)
    cf = jax.nn.silu(layernorm(cf_conv(u * jax.nn.sigmoid(v)), cf_ln_g, cf_ln_b))
    y_b = cf @ w_proj_b
    g_a, g_b = jnp.split(gates, 2, axis=-1)
    return (jax.nn.sigmoid(g_a) * y_a + jax.nn.sigmoid(g_b) * y_b) @ w_out


def swiglu(h, w_gate, w_up, w_down):
    return (jax.nn.silu(h @ w_gate) * (h @ w_up)) @ w_down


def setup_inputs(seed: int = 0) -> dict:
    key = jax.random.key(seed)
    ks = jax.random.split(key, 26)

    def nrm(k, shape, std):
        return jax.random.normal(k, shape, jnp.float32) * std

    x = nrm(ks[0], (BATCH, SEQ, D_MODEL), 1.0)
    c = nrm(ks[1], (BATCH, D_MODEL), 1.0)
    ctx = nrm(ks[2], (BATCH, CTX_LEN, D_MODEL), 1.0)
    c_ctx = nrm(ks[3], (D_MODEL,), 1.0)
    w_mod = nrm(ks[4], (DEPTH, D_MODEL, 6 * D_MODEL), 0.5 * D_MODEL ** -0.5)
    b_mod = nrm(ks[5], (DEPTH, 6 * D_MODEL), 0.02)
    norm_mix = 1.0 + nrm(ks[6], (DEPTH, D_MODEL), 0.02)
    w_in = nrm(ks[7], (DEPTH, D_MODEL, D_IN_TOTAL), D_MODEL ** -0.5)
    ssm_conv_w = nrm(ks[8], (DEPTH, SSM_CONV, CONV_DIM), SSM_CONV ** -0.5)
    ssm_conv_b = nrm(ks[9], (DEPTH, CONV_DIM), 0.02)
    dt0 = jnp.exp(jax.random.uniform(ks[10], (DEPTH, 2, SSM_HEADS), jnp.float32,
                                     minval=math.log(1e-3), maxval=math.log(1e-1)))
    dt_bias = dt0 + jnp.log(-jnp.expm1(-dt0))
    a_log = jnp.log(jax.random.uniform(ks[11], (DEPTH, 2, SSM_HEADS), jnp.float32,
                                       minval=1.0, maxval=16.0))
    d_skip = 1.0 + nrm(ks[12], (DEPTH, SSM_HEADS), 0.1)
    ssm_norm = 1.0 + nrm(ks[13], (DEPTH, D_INNER), 0.02)
    cf_conv_w = nrm(ks[14], (DEPTH, CF_KERNEL, D_CF), CF_KERNEL ** -0.5)
    cf_conv_b = nrm(ks[15], (DEPTH, D_CF), 0.02)
    cf_ln_g = 1.0 + nrm(ks[16], (DEPTH, D_CF), 0.02)
    cf_ln_b = nrm(ks[17], (DEPTH, D_CF), 0.02)
    w_proj_a = nrm(ks[18], (DEPTH, D_INNER, D_MODEL), D_INNER ** -0.5)
    w_proj_b = nrm(ks[19], (DEPTH, D_CF, D_MODEL), D_CF ** -0.5)
    w_out = nrm(ks[20], (DEPTH, D_MODEL, D_MODEL), D_MODEL ** -0.5)
    norm_ffn = 1.0 + nrm(ks[21], (DEPTH, D_MODEL), 0.02)
    w_ffn_gate = nrm(ks[22], (DEPTH, D_MODEL, D_FF), D_MODEL ** -0.5)
    w_ffn_up = nrm(ks[23], (DEPTH, D_MODEL, D_FF), D_MODEL ** -0.5)
    w_ffn_down = nrm(ks[24], (DEPTH, D_FF, D_MODEL), D_FF ** -0.5)
    norm_final = 1.0 + nrm(ks[25], (D_MODEL,), 0.02)
    return {"x": x, "c": c, "ctx": ctx, "c_ctx": c_ctx, "w_mod": w_mod, "b_mod": b_mod,
            "norm_mix": norm_mix, "w_in": w_in, "ssm_conv_w": ssm_conv_w, "ssm_conv_b": ssm_conv_b,
            "dt_bias": dt_bias, "a_log": a_log, "d_skip": d_skip, "ssm_norm": ssm_norm,
            "cf_conv_w": cf_conv_w, "cf_conv_b": cf_conv_b, "cf_ln_g": cf_ln_g, "cf_ln_b": cf_ln_b,
            "w_proj_a": w_proj_a, "w_proj_b": w_proj_b, "w_out": w_out, "norm_ffn": norm_ffn,
            "w_ffn_gate": w_ffn_gate, "w_ffn_up": w_ffn_up, "w_ffn_down": w_ffn_down,
            "norm_final": norm_final}


def reference(x, c, ctx, c_ctx, w_mod, b_mod, norm_mix, w_in, ssm_conv_w, ssm_conv_b, dt_bias,
              a_log, d_skip, ssm_norm, cf_conv_w, cf_conv_b, cf_ln_g, cf_ln_b, w_proj_a, w_proj_b,
              w_out, norm_ffn, w_ffn_gate, w_ffn_up, w_ffn_down, norm_final):
    b = x.shape[0]
    rows = x.shape[1] // GRID_W
    zero_state = jnp.zeros((b, SSM_GROUPS, HEADS_PER_GROUP, SSM_HEAD_DIM, SSM_STATE), jnp.float32)
    for l in range(DEPTH):
        sh1, sc1, g1, sh2, sc2, g2 = [t[:, None, :] for t in ada_params(c, w_mod[l], b_mod[l])]
        csh1, csc1, cg1, csh2, csc2, cg2 = ada_params(c_ctx, w_mod[l], b_mod[l])

        hc = modulate(rmsnorm(ctx, norm_mix[l]), csh1, csc1)
        ctx_parts, h_f, h_b = mixer_front(hc, w_in[l], ssm_conv_w[l], ssm_conv_b[l], dt_bias[l],
                                          a_log[l], zero_state, zero_state)

        hx = modulate(rmsnorm(x, norm_mix[l]), sh1, sc1)
        x_parts, _, _ = mixer_front(hx, w_in[l], ssm_conv_w[l], ssm_conv_b[l], dt_bias[l],
                                    a_log[l], h_f, h_b)
        x = x + g1 * mixer_back(
            x_parts, d_skip[l], ssm_norm[l],
            lambda t: dwconv_grid_columns(t, cf_conv_w[l], cf_conv_b[l], rows),
            cf_ln_g[l], cf_ln_b[l], w_proj_a[l], w_proj_b[l], w_out[l])
        hx = modulate(rmsnorm(x, norm_ffn[l]), sh2, sc2)
        x = x + g2 * swiglu(hx, w_ffn_gate[l], w_ffn_up[l], w_ffn_down[l])

        if l + 1 < DEPTH:
            ctx = ctx + cg1 * mixer_back(
                ctx_parts, d_skip[l], ssm_norm[l],
                lambda t: dwconv_seq(t, cf_conv_w[l], cf_conv_b[l]),
                cf_ln_g[l], cf_ln_b[l], w_proj_a[l], w_proj_b[l], w_out[l])
            hc = modulate(rmsnorm(ctx, norm_ffn[l]), csh2, csc2)
            ctx = ctx + cg2 * swiglu(hc, w_ffn_gate[l], w_ffn_up[l], w_ffn_down[l])
    return rmsnorm(x, norm_final)
```

```python
import contextlib
import numpy as np
import concourse.bass as bass
import concourse.mybir as mybir
from concourse.bass_utils import run_bass_kernel_spmd

F32 = mybir.dt.float32
BF16 = mybir.dt.bfloat16
AF = mybir.ActivationFunctionType
ALU = mybir.AluOpType
AX = mybir.AxisListType

SEM_MAX = 30000


class Tile:
    __slots__ = ("name", "w", "r")

    def __init__(self, name):
        self.name = name
        self.w = None
        self.r = {}


class Sched:
    def __init__(self, nc, n_dma_sems=24):
        self.nc = nc
        self.eng = {"pe": nc.tensor, "act": nc.scalar, "dve": nc.vector, "pool": nc.gpsimd, "sp": nc.sync}
        self.sems = {k: [] for k in self.eng}
        self.nsig = {k: 0 for k in self.eng}
        self.waited = {k: {} for k in self.eng}
        self.dma_ring = {}
        self.n_dma_sems = n_dma_sems
        self.ninst = 0
        self._semcache = {}

    def _new_sem(self, name):
        cm = self.nc.semaphore(name)
        s = cm.__enter__()
        return s

    def _eng_sem(self, e, idx):
        lst = self.sems[e]
        while len(lst) <= idx:
            lst.append(self._new_sem(f"s_{e}_{len(lst)}"))
        return lst[idx]

    def _need(self, e, ev, waits):
        if ev is None:
            return
        key, sem, val, src = ev
        if self.waited[e].get(key, 0) >= val:
            return
        prev = waits.get(key)
        if prev is None or prev[1] < val:
            waits[key] = (sem, val, src)

    def _emit_waits(self, e, waits):
        for key, (sem, val, src) in waits.items():
            if src == e and key[1] * SEM_MAX + val - 1 >= self.nsig[e]:
                raise RuntimeError(f"self-deadlock: {e} waits on its own unsignaled op {key} {val}")
            self.eng[e].wait_ge(sem, val)
            self.waited[e][key] = val
            self.ninst += 1

    def _deps(self, e, reads, writes):
        waits = {}
        for t in reads:
            self._need(e, t.w, waits)
        for t in writes:
            if t.w is not None and t.w[3] != e:
                self._need(e, t.w, waits)
            for k, ev in t.r.items():
                if ev[3] != e:
                    self._need(e, ev, waits)
        return waits

    def _update(self, ev, e, reads, writes):
        for t in reads:
            t.r[e if ev[3] == e else ev[0]] = ev
        for t in writes:
            t.w = ev
            t.r = {}

    def op(self, e, fn, reads=(), writes=(), sig=True):
        waits = self._deps(e, reads, writes)
        self._emit_waits(e, waits)
        ins = fn(self.eng[e])
        self.ninst += 1
        i = self.nsig[e]
        sidx, val = divmod(i, SEM_MAX)
        sem = self._eng_sem(e, sidx)
        if sig:
            ins.then_inc(sem, 1)
            self.nsig[e] += 1
        ev = ((e, sidx), sem, val + 1, e)
        self._update(ev, e, reads, writes)
        return ins

    def dma(self, q, out, in_, reads=(), writes=(), **kw):
        if q not in self.dma_ring:
            self.dma_ring[q] = {"sems": [self._new_sem(f"d_{q}_{i}") for i in range(self.n_dma_sems)], "n": 0}
        ring = self.dma_ring[q]
        j = ring["n"]
        ring["n"] += 1
        slot = j % self.n_dma_sems
        sem = ring["sems"][slot]
        cnt = j // self.n_dma_sems
        waits = self._deps(q, reads, writes)
        if cnt > 0:
            self._need(q, (("dma", q, slot), sem, 16 * cnt, None), waits)
        self._emit_waits(q, waits)
        ins = self.eng[q].dma_start(out=out, in_=in_, **kw)
        ins.then_inc(sem, 16)
        self.ninst += 1
        ev = (("dma", q, slot), sem, 16 * (cnt + 1), None)
        self._update(ev, None, reads, writes)
        return ins

    def wait_tiles(self, e, tiles):
        waits = {}
        for t in tiles:
            self._need(e, t.w, waits)
        self._emit_waits(e, waits)


D = 2048; L = 8192; LOWN = 4096; CTX = 256; NTOK = L + CTX
DI = 4096; NH = 64; HP = 64; NG = 8; NS = 128; CONVD = 6144; DFF = 5632
OFF_Z = 4096; OFF_XBC = OFF_Z + CONVD; OFF_DT = OFF_XBC + 128; OFF_GLU = OFF_DT + 4096; DIN = OFF_GLU + 4096
EPS = 1e-6


def build_nc(debug_outs=(), stop_after=99):
    nc = bass.Bass("TRN2", target_bir_lowering=False)
    def din(n, s): return nc.dram_tensor(n, list(s), F32, kind="ExternalInput").ap()
    def dscr(n, s):
        kind = "ExternalOutput" if n in debug_outs else "Internal"
        return nc.dram_tensor(n, list(s), F32, kind=kind).ap()
    xl = din("xl", [L, D]); ctxl = din("ctxl", [CTX, D]); c2T = din("c2T", [128, 2, 16])
    w_mod = din("w_mod", [D, 6 * D]); bmodT = din("bmodT", [128, 96])
    nmixT = din("nmixT", [128, 16]); nffnT = din("nffnT", [128, 16])
    w_in = din("w_in", [D, DIN]); w_dt = din("w_dt", [D, 128])
    cw = din("cw", [128, 48, 5]); cb = din("cb", [128, 48])
    dtb = din("dtb", [1, 128]); alog = din("alog", [1, 128])
    dsk = din("dsk", [1, DI]); ssmn = din("ssmn", [1, DI])
    cfw = din("cfw", [128, 16, 31]); cfb = din("cfb", [128, 16]); lng = din("lng", [128, 16]); lnb = din("lnb", [128, 16])
    w_pa = din("w_pa", [DI, D]); w_pb = din("w_pb", [D, D]); w_o = din("w_o", [D, D])
    w_g = din("w_g", [D, DFF]); w_u = din("w_u", [D, DFF]); w_d = din("w_d", [DFF, D])
    nfin = din("nfin", [1, D])
    out = nc.dram_tensor("out", [LOWN, D], F32, kind="ExternalOutput").ap()
    ZS = dscr("ZS", [LOWN, DI]); GATE = dscr("GATE", [LOWN, 2 * D]); PRE = dscr("PRE", [CONVD, NTOK])
    DT = dscr("DT", [NTOK, 128]); G = dscr("G", [D, 5120]); XS = dscr("XS", [NTOK, DI])
    BTOK = dscr("BTOK", [NTOK, 1024]); BT = dscr("BT", [1024, NTOK]); CT = dscr("CT", [1024, NTOK])
    Y1 = dscr("Y1", [LOWN, DI]); Y2 = dscr("Y2", [LOWN, DI]); GV = dscr("GV", [2, D])
    DBG = dscr("DBG", [5, 128, D]) if "DBG" in debug_outs else None
    def dbf(n, s): return nc.dram_tensor(n, list(s), BF16, kind="Internal").ap()
    w_in_b = dbf("w_in_b", [D, DIN]); w_dt_b = dbf("w_dt_b", [D, 128]); w_pa_b = dbf("w_pa_b", [DI, D]); w_pb_b = dbf("w_pb_b", [D, D])
    w_o_b = dbf("w_o_b", [D, D]); w_g_b = dbf("w_g_b", [D, DFF]); w_u_b = dbf("w_u_b", [D, DFF]); w_d_b = dbf("w_d_b", [DFF, D])
    S = Sched(nc)
    alltiles = []
    def T(n):
        t = Tile(n); alltiles.append(t); return t

    def barrier():
        evs = []
        for e in S.eng:
            if S.nsig[e] > 0:
                i = S.nsig[e] - 1
                sidx, val = divmod(i, SEM_MAX)
                evs.append(((e, sidx), S.sems[e][sidx], val + 1, e))
        for q, ring in S.dma_ring.items():
            n = ring["n"]
            for slot in range(min(n, S.n_dma_sems)):
                cnt = (n - 1 - slot) // S.n_dma_sems + 1
                evs.append((("dma", q, slot), ring["sems"][slot], 16 * cnt, None))
        for e in S.eng:
            waits = {}
            for ev in evs:
                if ev[3] != e:
                    S._need(e, ev, waits)
            S._emit_waits(e, waits)
        for t in alltiles:
            t.w = None; t.r = {}

    P = contextlib.ExitStack()
    def psb(n, s, dt=F32): return P.enter_context(nc.sbuf_tensor(n, list(s), dt))
    ident = psb("ident", [128, 128]); T_c = T("consts")
    ones = psb("ones", [128, 128])
    triA = psb("triA", [128, 128]); triB = psb("triB", [128, 128])
    strA = psb("strA", [128, 128]); strB = psb("strB", [128, 128])
    A1 = psb("A1", [128, 2, 16]); S1 = psb("S1", [128, 2, 16]); A2 = psb("A2", [128, 16]); S2 = psb("S2", [128, 16])
    g1bc = psb("g1bc", [128, D]); g2bc = psb("g2bc", [128, D])
    def cmask(t, init, fill, cmp, base, cm, pat):
        S.op("pool", lambda e: e.memset(t[:], init), writes=[T_c])
        S.op("pool", lambda e: e.affine_select(out=t[:], in_=t[:], pattern=[[pat, 128]], compare_op=cmp, fill=fill,
                                               base=base, channel_multiplier=cm), reads=[T_c], writes=[T_c])
    cmask(ident, 0.0, 1.0, ALU.not_equal, 0, 1, -1)
    cmask(triA, 1.0, 0.0, ALU.is_ge, 0, -1, 1)
    cmask(triB, 1.0, 0.0, ALU.is_ge, 0, 1, -1)
    cmask(strA, 1.0, 0.0, ALU.is_ge, -1, 1, -1)
    cmask(strB, 1.0, 0.0, ALU.is_ge, -1, -1, 1)
    S.op("pool", lambda e: e.memset(ones[:], 1.0), writes=[T_c])

    banks = [P.enter_context(nc.psum_tensor(f"bank{i}", [128, 512], F32)) for i in range(8)]
    T_b = [T(f"bank{i}") for i in range(8)]

    def rsqrt_col(E, col, scale, T_col, scratch, T_s):
        S.op("dve", lambda e: e.tensor_scalar(out=col, in0=col, scalar1=scale, scalar2=EPS, op0=ALU.mult, op1=ALU.add),
             reads=[T_col], writes=[T_col])
        S.op("act", lambda e: e.activation(out=col, in_=col, func=AF.Ln), reads=[T_col], writes=[T_col])
        S.op("act", lambda e: e.activation(out=col, in_=col, func=AF.Exp, scale=-0.5), reads=[T_col], writes=[T_col])

    def x_to_hT(xt, T_x, Acol, Scol, hT, T_h, sub, junk, T_j, ssc, T_ss):
        S.op("act", lambda e: e.activation(out=junk[:], in_=xt[:], func=AF.Square, accum_out=ssc[:, 0:1]),
             reads=[T_x], writes=[T_j, T_ss])
        rsqrt_col(None, ssc[:, 0:1], 1.0 / D, T_ss, None, None)
        S.op("dve", lambda e: e.tensor_scalar(out=junk[:], in0=xt[:], scalar1=ssc[:, 0:1], scalar2=None, op0=ALU.mult),
             reads=[T_x, T_ss], writes=[T_j])
        for b4 in range(4):
            for q in range(4):
                kc = b4 * 4 + q
                S.op("pe", lambda e, kc=kc, b4=b4, q=q: e.transpose(banks[b4][:, q * 128:(q + 1) * 128],
                                                                   junk[:, kc * 128:(kc + 1) * 128], ident[:]),
                     reads=[T_j, T_c], writes=[T_b[b4]], sig=(q == 3))
            for q in range(4):
                kc = b4 * 4 + q
                S.op("dve", lambda e, kc=kc, b4=b4, q=q: e.tensor_scalar(
                    out=hT[:, kc, sub * 128:(sub + 1) * 128], in0=banks[b4][:, q * 128:(q + 1) * 128],
                    scalar1=Acol[:, kc:kc + 1], scalar2=Scol[:, kc:kc + 1], op0=ALU.mult, op1=ALU.add),
                    reads=[T_b[b4], T_c], writes=[T_h])

    wctr = [0]
    def gemm(actT, T_a, KC, Tn, pieces_fn, nblocks, mode, evac, wbufs, T_w, KT=16, bank0=0):
        for n in range(nblocks):
            pieces = pieces_fn(n)
            w = sum(p.shape[1] for p in pieces)
            nkt = (KC + KT - 1) // KT
            if mode == "tok":
                nps = Tn // 128
            else:
                nps = w // 128
            pb = [(bank0 + j) % 8 for j in range(nps)]
            for kt in range(nkt):
                k0 = kt * KT; kn = min(KT, KC - k0)
                bi = wctr[0] % len(wbufs); wctr[0] += 1
                wt = wbufs[bi]; Tw = T_w[bi]
                c0 = 0
                for p in pieces:
                    wi = p.shape[1]
                    S.dma("sp", wt[:, 0:kn, c0:c0 + wi], p[k0 * 128:(k0 + kn) * 128, :].rearrange("(kc p) n -> p kc n", p=128),
                          writes=[Tw])
                    c0 += wi
                for j in range(nps):
                    for kc in range(kn):
                        first = (kt == 0 and kc == 0); last = (kt == nkt - 1 and kc == kn - 1)
                        if mode == "tok":
                            S.op("pe", lambda e, j=j, kc=kc, first=first, last=last: e.matmul(
                                banks[pb[j]][:, 0:w], lhsT=actT[:, k0 + kc, j * 128:(j + 1) * 128], rhs=wt[:, kc, 0:w],
                                start=first, stop=last), reads=[T_a, Tw], writes=[T_b[pb[j]]], sig=(kc == kn - 1))
                        else:
                            S.op("pe", lambda e, j=j, kc=kc, first=first, last=last: e.matmul(
                                banks[pb[j]][:, 0:Tn], lhsT=wt[:, kc, j * 128:(j + 1) * 128], rhs=actT[:, k0 + kc, 0:Tn],
                                start=first, stop=last), reads=[T_a, Tw], writes=[T_b[pb[j]]], sig=(kc == kn - 1))
            for j in range(nps):
                evac(n, j, banks[pb[j]], T_b[pb[j]], w)
            if mode == "tok" and Tn // 128 <= 4:
                bank0 = (bank0 + 4) % 8
            elif mode == "feat":
                bank0 = (bank0 + 4) % 8
        return bank0

    with contextlib.ExitStack() as es:
        def sb(n, s, dt=F32): return es.enter_context(nc.sbuf_tensor("p0_" + n, list(s), dt))
        c2 = sb("c2", [128, 2, 16]); T_c2 = T("c2")
        scT = sb("scT", [128, 16, 2]); T_sc = T("scT")
        wts = [sb(f"wm{i}", [128, 16, 512]) for i in range(2)]; T_wm = [T("wm0"), T("wm1")]
        mod = sb("mod", [128, 96, 2]); T_mod = T("mod")
        bm = sb("bm", [128, 96]); nm = sb("nm", [128, 16]); nf = sb("nf", [128, 16]); T_sm = T("small")
        tmp = sb("tmp", [128, 2, 16]); T_tmp = T("tmp")
        gc = sb("gc", [128, 2, 16]); T_gc = T("gc")
        S.dma("sp", c2[:], c2T, writes=[T_c2])
        S.dma("sp", bm[:], bmodT, writes=[T_sm]); S.dma("sp", nm[:], nmixT, writes=[T_sm]); S.dma("sp", nf[:], nffnT, writes=[T_sm])
        for i in range(2):
            S.op("act", lambda e, i=i: e.activation(out=scT[:, :, i], in_=c2[:, i, :], func=AF.Silu), reads=[T_c2], writes=[T_sc])
        pm = banks[0]
        for n in range(24):
            wt = wts[n % 2]; Tw = T_wm[n % 2]
            S.dma("sp", wt[:], w_mod[:, n * 512:(n + 1) * 512].rearrange("(kc p) n -> p kc n", p=128), writes=[Tw])
            for cs in range(4):
                j = n * 4 + cs
                for kc in range(16):
                    S.op("pe", lambda e, j=j, kc=kc, cs=cs, wt=wt: e.matmul(pm[:, 2 * j:2 * j + 2], lhsT=wt[:, kc, cs * 128:(cs + 1) * 128],
                                                                         rhs=scT[:, kc, :], start=(kc == 0), stop=(kc == 15)),
                         reads=[Tw, T_sc], writes=[T_b[0]], sig=(kc == 15))
        S.op("dve", lambda e: e.tensor_tensor(out=mod[:], in0=pm[:, 0:192].rearrange("p (j i) -> p j i", i=2),
                                              in1=bm[:].unsqueeze(2).to_broadcast([128, 96, 2]), op=ALU.add),
             reads=[T_b[0], T_sm], writes=[T_mod])
        def modv(k, i):
            return mod[:, 16 * k:16 * (k + 1), i]
        for i in range(2):
            S.op("dve", lambda e, i=i: e.tensor_scalar(out=tmp[:, i, :], in0=modv(1, i), scalar1=1.0, scalar2=None, op0=ALU.add),
                 reads=[T_mod], writes=[T_tmp])
            S.op("dve", lambda e, i=i: e.tensor_tensor(out=A1[:, i, :], in0=tmp[:, i, :], in1=nm[:], op=ALU.mult),
                 reads=[T_tmp, T_sm], writes=[T_c])
            S.op("dve", lambda e, i=i: e.tensor_copy(S1[:, i, :], modv(0, i)), reads=[T_mod], writes=[T_c])
        S.op("dve", lambda e: e.tensor_scalar(out=tmp[:, 0, :], in0=modv(4, 0), scalar1=1.0, scalar2=None, op0=ALU.add),
             reads=[T_mod], writes=[T_tmp])
        S.op("dve", lambda e: e.tensor_tensor(out=A2[:], in0=tmp[:, 0, :], in1=nf[:], op=ALU.mult), reads=[T_tmp, T_sm], writes=[T_c])
        S.op("dve", lambda e: e.tensor_copy(S2[:], modv(3, 0)), reads=[T_mod], writes=[T_c])
        S.op("dve", lambda e: e.tensor_copy(gc[:, 0, :], modv(2, 0)), reads=[T_mod], writes=[T_gc])
        S.op("dve", lambda e: e.tensor_copy(gc[:, 1, :], modv(5, 0)), reads=[T_mod], writes=[T_gc])
        T_gv = T("GV")
        for i in range(2):
            S.dma("sp", GV[i:i + 1, :].rearrange("o (j p) -> p (o j)", p=128), gc[:, i, :], reads=[T_gc], writes=[T_gv],
                  allow_slow_non_contiguous=True)
        S.dma("sp", g1bc[:], GV[0:1, :].partition_broadcast(128), reads=[T_gv], writes=[T_c])
        S.dma("sp", g2bc[:], GV[1:2, :].partition_broadcast(128), reads=[T_gv], writes=[T_c])
        barrier()

    T_wsc = T("wscratch")
    with contextlib.ExitStack() as es:
        def sb(n, s, dt=F32): return es.enter_context(nc.sbuf_tensor("pc_" + n, list(s), dt))
        cin = [sb(f"cin{i}", [128, 2048]) for i in range(3)]; T_cin = [T(f"cin{i}") for i in range(3)]
        cout = [sb(f"cout{i}", [128, 2048], BF16) for i in range(3)]; T_cout = [T(f"cout{i}") for i in range(3)]
        cc = 0
        for src, dst in [(w_in, w_in_b), (w_dt, w_dt_b), (w_pa, w_pa_b), (w_pb, w_pb_b), (w_o, w_o_b), (w_g, w_g_b), (w_u, w_u_b), (w_d, w_d_b)]:
            Kd, Nd = src.shape
            for r in range(Kd // 128):
                for c0 in range(0, Nd, 2048):
                    w = min(2048, Nd - c0)
                    ci = cin[cc % 3]; Tci = T_cin[cc % 3]; co = cout[cc % 3]; Tco = T_cout[cc % 3]
                    S.dma("sp", ci[:, 0:w], src[r * 128:(r + 1) * 128, c0:c0 + w], writes=[Tci])
                    if cc % 2:
                        S.op("act", lambda e: e.activation(out=co[:, 0:w], in_=ci[:, 0:w], func=AF.Copy), reads=[Tci], writes=[Tco])
                    else:
                        S.op("dve", lambda e: e.tensor_copy(co[:, 0:w], ci[:, 0:w]), reads=[Tci], writes=[Tco])
                    S.dma("sp", dst[r * 128:(r + 1) * 128, c0:c0 + w], co[:, 0:w], reads=[Tco], writes=[T_wsc])
                    cc += 1
        barrier()

    T_scr = {n: T(n) for n in ["ZS", "GATE", "PRE", "DT", "G", "XS", "BTOK", "BT", "CT", "Y1", "Y2"]}
    with contextlib.ExitStack() as es:
      if stop_after >= 1:
        def sb(n, s, dt=F32): return es.enter_context(nc.sbuf_tensor("p1_" + n, list(s), dt))
        hT = sb("hT", [128, 16, 512], BF16); T_h = T("hT")
        xin = [sb(f"xin{i}", [128, D]) for i in range(2)]; T_xin = [T("xin0"), T("xin1")]
        junk = sb("junk", [128, D]); T_j = T("junk")
        ssc = sb("ssc", [128, 1]); T_ss = T("ssc")
        wb = [sb(f"wb{i}", [128, 16, 512], BF16) for i in range(3)]; T_wb = [T("wb0"), T("wb1"), T("wb2")]
        stg = [sb(f"stg{i}", [128, 512]) for i in range(4)]; T_stg = [T(f"stg{i}") for i in range(4)]
        sig = [sb(f"sig{i}", [128, 512]) for i in range(2)]; T_sig = [T(f"sig{i}") for i in range(2)]
        dtbb = sb("dtbb", [128, 128]); T_dtbb = T("dtbb")
        S.dma("sp", dtbb[:], dtb.partition_broadcast(128), writes=[T_dtbb])
        sctr = [0]
        def stage():
            i = sctr[0] % 4; sctr[0] += 1
            return stg[i], T_stg[i]
        tiles = [("ctx", L, 256)] + [("lat", t * 512, 512) for t in range(16)]
        xctr = 0
        for kind, tok0, Tn in tiles:
            own = kind == "lat" and tok0 < LOWN
            mi = 1 if kind == "ctx" else 0
            for sub in range(Tn // 128):
                xt = xin[xctr % 2]; Tx = T_xin[xctr % 2]; xctr += 1
                src = ctxl[sub * 128:(sub + 1) * 128, :] if kind == "ctx" else xl[tok0 + sub * 128: tok0 + (sub + 1) * 128, :]
                S.dma("sp", xt[:], src, writes=[Tx])
                x_to_hT(xt, Tx, A1[:, mi, :], S1[:, mi, :], hT, T_h, sub, junk, T_j, ssc, T_ss)
            bank0 = 0
            def ev_pre(c_off):
                def f(n, j, bk, Tb, w):
                    st, Ts = stage()
                    S.op("act" if j % 2 else "dve", (lambda e: e.activation(out=st[:, 0:Tn], in_=bk[:, 0:Tn], func=AF.Copy)) if j % 2 else
                         (lambda e: e.tensor_copy(st[:, 0:Tn], bk[:, 0:Tn])), reads=[Tb], writes=[Ts])
                    r0 = c_off + n * 512 + j * 128
                    S.dma("sp", PRE[r0:r0 + 128, tok0:tok0 + Tn], st[:, 0:Tn], reads=[Ts], writes=[T_scr["PRE"]])
                return f
            nb = 12 if (own or (kind == "lat" and tok0 == LOWN)) else 10
            bank0 = gemm(hT, T_h, 16, Tn, lambda n: [w_in_b[:, OFF_Z + n * 512: OFF_Z + (n + 1) * 512]], nb, "feat", ev_pre(0), wb, T_wb, bank0=bank0)
            def ev_dt(n, j, bk, Tb, w):
                st, Ts = stage()
                S.op("dve", lambda e: e.tensor_tensor(out=st[:, 0:128], in0=bk[:, 0:128], in1=dtbb[:], op=ALU.add), reads=[Tb, T_dtbb], writes=[Ts])
                S.op("act", lambda e: e.activation(out=st[:, 0:128], in_=st[:, 0:128], func=AF.Exp), reads=[Ts], writes=[Ts])
                S.op("dve", lambda e: e.tensor_scalar(out=st[:, 0:128], in0=st[:, 0:128], scalar1=1.0, scalar2=None, op0=ALU.add), reads=[Ts], writes=[Ts])
                S.op("act", lambda e: e.activation(out=st[:, 0:128], in_=st[:, 0:128], func=AF.Ln), reads=[Ts], writes=[Ts])
                S.dma("sp", DT[tok0 + j * 128: tok0 + (j + 1) * 128, :], st[:, 0:128], reads=[Ts], writes=[T_scr["DT"]])
            bank0 = gemm(hT, T_h, 16, Tn, lambda n: [w_dt_b[:, :]], 1, "tok", ev_dt, wb, T_wb, bank0=bank0)
            if kind == "lat" and tok0 < 5120:
                def ev_glu(n, j, bk, Tb, w):
                    if j < 2:
                        return
                    jj = j - 2
                    ubk = banks[(T_b.index(Tb) - 2) % 8]; Tub = T_b[(T_b.index(Tb) - 2) % 8]
                    sg = sig[jj]; Tsg = T_sig[jj]
                    S.op("act", lambda e: e.activation(out=sg[:, 0:Tn], in_=bk[:, 0:Tn], func=AF.Sigmoid), reads=[Tb], writes=[Tsg])
                    st, Ts = stage()
                    S.op("dve", lambda e: e.tensor_tensor(out=st[:, 0:Tn], in0=ubk[:, 0:Tn], in1=sg[:, 0:Tn], op=ALU.mult), reads=[Tub, Tsg], writes=[Ts])
                    r0 = (2 * n + jj) * 128
                    S.dma("sp", G[r0:r0 + 128, tok0:tok0 + Tn], st[:, 0:Tn], reads=[Ts], writes=[T_scr["G"]])
                bank0 = gemm(hT, T_h, 16, Tn, lambda n: [w_in_b[:, OFF_DT + n * 256: OFF_DT + (n + 1) * 256],
                                                          w_in_b[:, OFF_DT + D + n * 256: OFF_DT + D + (n + 1) * 256]], 8, "feat", ev_glu, wb, T_wb, bank0=bank0)
            if own:
                def ev_act(dst, func, c_off):
                    def f(n, j, bk, Tb, w):
                        st, Ts = stage()
                        S.op("act", lambda e: e.activation(out=st[:], in_=bk[:], func=func), reads=[Tb], writes=[Ts])
                        r0 = tok0 + j * 128
                        S.dma("sp", dst[r0:r0 + 128, n * 512:(n + 1) * 512], st[:], reads=[Ts], writes=[T_scr["ZS" if dst is ZS else "GATE"]])
                    return f
                bank0 = gemm(hT, T_h, 16, Tn, lambda n: [w_in_b[:, n * 512:(n + 1) * 512]], 8, "tok", ev_act(ZS, AF.Silu, 0), wb, T_wb, bank0=bank0)
                bank0 = gemm(hT, T_h, 16, Tn, lambda n: [w_in_b[:, OFF_GLU + n * 512: OFF_GLU + (n + 1) * 512]], 8, "tok", ev_act(GATE, AF.Sigmoid, 0), wb, T_wb, bank0=bank0)
        barrier()

    with contextlib.ExitStack() as es:
      if stop_after >= 2:
        def sb(n, s, dt=F32): return es.enter_context(nc.sbuf_tensor("p2_" + n, list(s), dt))
        cwt = sb("cwt", [128, 48, 5]); cbt = sb("cbt", [128, 48]); T_cw = T("cw")
        S.dma("sp", cwt[:], cw, writes=[T_cw]); S.dma("sp", cbt[:], cb, writes=[T_cw])
        pin = [sb(f"pin{i}", [128, 516]) for i in range(3)]; T_pin = [T(f"pin{i}") for i in range(3)]
        acc = [sb(f"acc{i}", [128, 512]) for i in range(2)]; T_acc = [T(f"acc{i}") for i in range(2)]
        fo = [sb(f"fo{i}", [128, 512]) for i in range(3)]; T_fo = [T(f"fo{i}") for i in range(3)]
        tk = [sb(f"tk{i}", [128, 4, 128]) for i in range(2)]; T_tk = [T(f"tk{i}") for i in range(2)]
        ctr = 0
        segs = [(L, L + CTX)] + [(0, L)]
        for (s0, s1) in segs:
            for tok0 in range(s0, s1, 512):
                Tn = min(512, s1 - tok0)
                own = (s0 == 0 and tok0 < LOWN)
                for j in range(48):
                    if j >= 40 and not own:
                        continue
                    pi = pin[ctr % 3]; Tp = T_pin[ctr % 3]
                    ac = acc[ctr % 2]; Ta = T_acc[ctr % 2]
                    f = fo[ctr % 3]; Tf = T_fo[ctr % 3]
                    lo = max(tok0 - 2, s0); hi = min(tok0 + Tn + 2, s1)
                    if lo > tok0 - 2:
                        S.op("pool", lambda e: e.memset(pi[:, 0:2], 0.0), writes=[Tp])
                    if hi < tok0 + Tn + 2:
                        S.op("pool", lambda e: e.memset(pi[:, Tn + 2:Tn + 4], 0.0), writes=[Tp])
                    S.dma("sp", pi[:, lo - (tok0 - 2): hi - (tok0 - 2)], PRE[j * 128:(j + 1) * 128, lo:hi], reads=[T_scr["PRE"]], writes=[Tp])
                    S.op("dve", lambda e: e.tensor_scalar(out=ac[:, 0:Tn], in0=pi[:, 0:Tn], scalar1=cwt[:, j, 0:1], scalar2=None, op0=ALU.mult),
                         reads=[Tp, T_cw], writes=[Ta])
                    for k in range(1, 5):
                        S.op("dve", lambda e, k=k: e.scalar_tensor_tensor(out=ac[:, 0:Tn], in0=pi[:, k:k + Tn], scalar=cwt[:, j, k:k + 1], in1=ac[:, 0:Tn],
                                                                        op0=ALU.mult, op1=ALU.add), reads=[Tp, T_cw, Ta], writes=[Ta])
                    S.op("act", lambda e: e.activation(out=f[:, 0:Tn], in_=ac[:, 0:Tn], func=AF.Silu, bias=cbt[:, j:j + 1]), reads=[Ta, T_cw], writes=[Tf])
                    if j >= 32:
                        dst, nm_, g = (BT, "BT", j - 32) if j < 40 else (CT, "CT", j - 40)
                        S.dma("sp", dst[g * 128:(g + 1) * 128, tok0:tok0 + Tn], f[:, 0:Tn], reads=[Tf], writes=[T_scr[nm_]])
                    if j < 40:
                        bi = ctr % 2
                        nsub = Tn // 128
                        for sub in range(nsub):
                            S.op("pe", lambda e, sub=sub: e.transpose(banks[bi][:, sub * 128:(sub + 1) * 128], f[:, sub * 128:(sub + 1) * 128], ident[:]),
                                 reads=[Tf, T_c], writes=[T_b[bi]], sig=(sub == nsub - 1))
                        t_ = tk[bi]; Tt = T_tk[bi]
                        S.op("act", lambda e: e.activation(out=t_[:, 0:nsub, :], in_=banks[bi][:, 0:nsub * 128].rearrange("p (s c) -> p s c", c=128), func=AF.Copy),
                             reads=[T_b[bi]], writes=[Tt])
                        if j < 32:
                            S.dma("sp", XS[tok0:tok0 + Tn, j * 128:(j + 1) * 128].rearrange("(s p) c -> p s c", p=128), t_[:, 0:nsub, :],
                                  reads=[Tt], writes=[T_scr["XS"]])
                        else:
                            g = j - 32
                            S.dma("sp", BTOK[tok0:tok0 + Tn, g * 128:(g + 1) * 128].rearrange("(s p) c -> p s c", p=128), t_[:, 0:nsub, :],
                                  reads=[Tt], writes=[T_scr["BTOK"]])
                    ctr += 1
        barrier()

    with contextlib.ExitStack() as es:
      if stop_after >= 3:
        def sb(n, s, dt=F32): return es.enter_context(nc.sbuf_tensor("p3_" + n, list(s), dt))
        abc = sb("abc", [128, 128]); T_abc = T("abc")
        S.dma("sp", abc[:], alog.partition_broadcast(128), writes=[T_abc])
        S.op("act", lambda e: e.activation(out=abc[:], in_=abc[:], func=AF.Exp), reads=[T_abc], writes=[T_abc])
        S.op("dve", lambda e: e.tensor_scalar(out=abc[:], in0=abc[:], scalar1=-1.0, scalar2=None, op0=ALU.mult), reads=[T_abc], writes=[T_abc])
        hst = sb("hst", [128, 8, 512]); T_hst = T("hst")
        xs_t = [sb("xs0", [128, 64, 64])] * 2; T_xs = [T("xs0")] * 2
        bk_t = [sb(f"bk{i}", [128, 8, 128]) for i in range(2)]; T_bk = [T(f"bk{i}") for i in range(2)]
        bt_t = [sb(f"bt{i}", [128, 8, 128]) for i in range(2)]; T_bt = [T(f"bt{i}") for i in range(2)]
        ct_t = [sb(f"ct{i}", [128, 8, 128]) for i in range(2)]; T_ct = [T(f"ct{i}") for i in range(2)]
        dt_t = [sb(f"dt{i}", [128, 64]) for i in range(2)]; T_dt = [T(f"dt{i}") for i in range(2)]
        da = sb("da", [128, 64]); T_da = T("da")
        sm = sb("sm", [128, 4, 64]); T_sm = T("sm")
        xdt = sb("xdt", [128, 64, 64]); T_xdt = T("xdt")
        xde = sb("xde", [128, 64, 64]); T_xde = T("xde")
        Bm = sb("Bm", [128, 64, 128]); T_Bm = T("Bm")
        cbm = [sb(f"cbm{i}", [128, 128]) for i in range(2)]; T_cbm = [T(f"cbm{i}") for i in range(2)]
        E_t = [sb(f"E{i}", [128, 4, 128]) for i in range(2)]; T_E = [T(f"E{i}") for i in range(2)]
        M_t = [sb(f"M{i}", [128, 4, 128]) for i in range(2)]; T_M = [T(f"M{i}") for i in range(2)]
        yo = sb("yo", [128, DI]); T_yo = T("yo")
        tq = sb("tq", [128, 512]); T_tq = T("tq")
        cctr = 0
        for d in range(2):
            tri = triA if d == 0 else triB
            stri = strA if d == 0 else strB
            Yd, Ynm = (Y1, "Y1") if d == 0 else (Y2, "Y2")
            S.op("pool", lambda e: e.memset(hst[:], 0.0), reads=[], writes=[T_hst])
            if d == 0:
                order = [(L, False), (L + 128, False)] + [(c * 128, True) for c in range(32)]
            else:
                order = [(L + 128, False), (L, False)] + [(c * 128, False) for c in range(63, 31, -1)] + [(c * 128, True) for c in range(31, -1, -1)]
            for tok0, full in order:
                i2 = cctr % 2; cctr += 1
                xs = xs_t[i2]; Txs = T_xs[i2]; bk = bk_t[i2]; Tbk = T_bk[i2]; dtt = dt_t[i2]; Tdt = T_dt[i2]
                bt = bt_t[i2]; Tbt = T_bt[i2]; ct = ct_t[i2]; Tct = T_ct[i2]
                S.dma("sp", xs[:].rearrange("p h c -> p (h c)"), XS[tok0:tok0 + 128, :], reads=[T_scr["XS"]], writes=[Txs])
                S.dma("sp", bk[:].rearrange("p g n -> p (g n)"), BTOK[tok0:tok0 + 128, :], reads=[T_scr["BTOK"]], writes=[Tbk])
                S.dma("sp", dtt[:], DT[tok0:tok0 + 128, d * 64:(d + 1) * 64], reads=[T_scr["DT"]], writes=[Tdt])
                if full:
                    S.dma("sp", bt[:], BT[:, tok0:tok0 + 128].rearrange("(g n) t -> n g t", n=128), reads=[T_scr["BT"]], writes=[Tbt])
                    S.dma("sp", ct[:], CT[:, tok0:tok0 + 128].rearrange("(g n) t -> n g t", n=128), reads=[T_scr["CT"]], writes=[Tct])
                S.op("dve", lambda e: e.tensor_tensor(out=da[:], in0=dtt[:], in1=abc[:, d * 64:(d + 1) * 64], op=ALU.mult), reads=[Tdt, T_abc], writes=[T_da])
                pcs = banks[7]; Tpcs = T_b[7]
                S.op("pe", lambda e: e.matmul(pcs[:, 0:64], lhsT=tri[:], rhs=da[:], start=True, stop=True), reads=[T_c, T_da], writes=[Tpcs], sig=False)
                S.op("pe", lambda e: e.matmul(pcs[:, 64:128], lhsT=ones[:], rhs=da[:], start=True, stop=True), reads=[T_c, T_da], writes=[Tpcs])
                S.op("act", lambda e: e.activation(out=sm[:, 0, :], in_=pcs[:, 0:64], func=AF.Exp), reads=[Tpcs], writes=[T_sm])
                S.op("act", lambda e: e.activation(out=sm[:, 2, :], in_=pcs[:, 64:128], func=AF.Exp), reads=[Tpcs], writes=[T_sm])
                S.op("act", lambda e: e.activation(out=sm[:, 3, :], in_=pcs[:, 64:128], func=AF.Copy), reads=[Tpcs], writes=[T_sm])
                S.op("dve", lambda e: e.tensor_tensor(out=sm[:, 3, :], in0=sm[:, 3, :], in1=pcs[:, 0:64], op=ALU.subtract), reads=[Tpcs, T_sm], writes=[T_sm])
                S.op("act", lambda e: e.activation(out=sm[:, 1, :], in_=sm[:, 3, :], func=AF.Exp), reads=[T_sm], writes=[T_sm])
                S.op("dve", lambda e: e.tensor_tensor(out=xdt[:], in0=xs[:], in1=dtt[:].unsqueeze(2).to_broadcast([128, 64, 64]), op=ALU.mult),
                     reads=[Txs, Tdt], writes=[T_xdt])
                S.op("dve", lambda e: e.tensor_tensor(out=xde[:], in0=xdt[:], in1=sm[:, 1, :].unsqueeze(2).to_broadcast([128, 64, 64]), op=ALU.mult),
                     reads=[T_xdt, T_sm], writes=[T_xde])
                if full:
                    S.op("pool", lambda e: e.tensor_tensor(out=Bm[:], in0=tri[:].unsqueeze(1).to_broadcast([128, 64, 128]),
                                                           in1=da[:].unsqueeze(2).to_broadcast([128, 64, 128]), op=ALU.mult),
                         reads=[T_c, T_da], writes=[T_Bm])
                    for g in range(8):
                        pcb = banks[6]; Tpcb = T_b[6]
                        S.op("pe", lambda e: e.matmul(pcb[:, 0:128], lhsT=bt[:, g, :], rhs=ct[:, g, :], start=True, stop=True), reads=[Tbt, Tct], writes=[Tpcb])
                        cm = cbm[g % 2]; Tcm = T_cbm[g % 2]
                        S.op("dve", lambda e: e.tensor_tensor(out=cm[:], in0=pcb[:, 0:128], in1=tri[:], op=ALU.mult), reads=[Tpcb, T_c], writes=[Tcm])
                        py = banks[g % 2]; Tpy = T_b[g % 2]
                        for hh in range(2):
                            h0 = g * 8 + hh * 4
                            pseg = banks[2 + hh]; Tps = T_b[2 + hh]
                            S.op("pe", lambda e: e.matmul(pseg[:], lhsT=stri[:], rhs=Bm[:, h0:h0 + 4, :].rearrange("p h l -> p (h l)"), start=True, stop=True),
                                 reads=[T_c, T_Bm], writes=[Tps])
                            Et = E_t[hh]; TE = T_E[hh]; Mt = M_t[hh]; TM = T_M[hh]
                            S.op("act", lambda e: e.activation(out=Et[:].rearrange("p h l -> p (h l)"), in_=pseg[:], func=AF.Exp), reads=[Tps], writes=[TE])
                            S.op("dve", lambda e: e.tensor_tensor(out=Mt[:], in0=Et[:], in1=cm[:].unsqueeze(1).to_broadcast([128, 4, 128]), op=ALU.mult),
                                 reads=[TE, Tcm], writes=[TM])
                            for q in range(4):
                                h = h0 + q; hl = hh * 4 + q
                                S.op("pe", lambda e, q=q, h=h, hl=hl: e.matmul(py[:, hl * 64:(hl + 1) * 64], lhsT=Mt[:, q, :], rhs=xdt[:, h, :], start=True, stop=True),
                                     reads=[TM, T_xdt], writes=[Tpy], sig=(q == 3))
                        pyo = banks[4 + g % 2]; Tpyo = T_b[4 + g % 2]
                        S.op("pe", lambda e: e.matmul(pyo[:], lhsT=ct[:, g, :], rhs=hst[:, g, :], start=True, stop=True), reads=[Tct, T_hst], writes=[Tpyo])
                        S.op("dve", lambda e: e.tensor_tensor(out=tq[:].rearrange("p (h c) -> p h c", c=64), in0=pyo[:].rearrange("p (h c) -> p h c", c=64),
                                                              in1=sm[:, 0, g * 8:(g + 1) * 8].unsqueeze(2).to_broadcast([128, 8, 64]), op=ALU.mult),
                             reads=[Tpyo, T_sm], writes=[T_tq])
                        S.op("dve", lambda e: e.tensor_tensor(out=yo[:, g * 512:(g + 1) * 512], in0=tq[:], in1=py[:], op=ALU.add), reads=[T_tq, Tpy], writes=[T_yo])
                    S.dma("sp", Yd[tok0:tok0 + 128, :], yo[:], reads=[T_yo], writes=[T_scr[Ynm]])
                for g in range(8):
                    ph = banks[4 + g % 2]; Tph = T_b[4 + g % 2]
                    S.op("pe", lambda e: e.matmul(ph[:], lhsT=bk[:, g, :], rhs=xde[:, g * 8:(g + 1) * 8, :].rearrange("p h c -> p (h c)"), start=True, stop=True),
                         reads=[Tbk, T_xde], writes=[Tph])
                    S.op("dve", lambda e: e.tensor_tensor(out=hst[:, g, :].rearrange("p (h c) -> p h c", c=64), in0=hst[:, g, :].rearrange("p (h c) -> p h c", c=64),
                                                          in1=sm[:, 2, g * 8:(g + 1) * 8].unsqueeze(2).to_broadcast([128, 8, 64]), op=ALU.mult),
                         reads=[T_hst, T_sm], writes=[T_hst])
                    S.op("dve", lambda e: e.tensor_tensor(out=hst[:, g, :], in0=hst[:, g, :], in1=ph[:], op=ALU.add), reads=[T_hst, Tph], writes=[T_hst])
        barrier()

    with contextlib.ExitStack() as es:
      if stop_after >= 4:
        def sb(n, s, dt=F32): return es.enter_context(nc.sbuf_tensor("p4_" + n, list(s), dt))
        nfbc = sb("nfbc", [128, D]); T_k = T("bconst")
        bq = sb("bq", [128, 512]); T_bq = T("bq")
        S.dma("sp", nfbc[:], nfin.partition_broadcast(128), writes=[T_k])
        cfwt = sb("cfwt", [128, 16, 31]); cfbt = sb("cfbt", [128, 16]); lngt = sb("lngt", [128, 16]); lnbt = sb("lnbt", [128, 16])
        S.dma("sp", cfwt[:], cfw, writes=[T_k]); S.dma("sp", cfbt[:], cfb, writes=[T_k])
        S.dma("sp", lngt[:], lng, writes=[T_k]); S.dma("sp", lnbt[:], lnb, writes=[T_k])
        qb = [[sb(f"q{k}", [128, 512]) for k in range(4)]] * 2; T_q = [[T(f"q{k}") for k in range(4)]] * 2
        yg = sb("yg", [128, DI]); T_yg = T("yg")
        ssq = sb("ssq", [128, 8]); T_ssq = T("ssq"); ssq8 = sb("ssq8", [128, 8])
        gw = [sb(f"gw{i}", [128, 2048]) for i in range(2)]; T_gw = [T(f"gw{i}") for i in range(2)]
        cf = sb("cf", [128, 16, 128]); T_cf = T("cf")
        sq = sb("sq", [128, 16, 128]); T_sq = T("sq")
        st4 = sb("st4", [128, 4, 128]); T_st4 = T("st4")
        mrg = sb("mrg", [128, D]); T_mrg = T("mrg")
        mT = sb("mT", [128, 16, 128], BF16); T_mT = T("mT"); cfT = mT; T_cfT = T_mT; hxT = mT; T_hxT = T_mT
        x1 = sb("x1", [128, D]); T_x1 = T("x1")
        actT = sb("actT", [128, 44, 128], BF16); T_actT = T("actT"); ygT = actT; T_ygT = T_actT
        wb = [sb(f"wb{i}", [128, 16, 512], BF16) for i in range(3)]; T_wb = [T("wb0"), T("wb1"), T("wb2")]
        gt = [sb(f"gt{i}", [128, 512]) for i in range(2)]; T_gt = [T(f"gt{i}") for i in range(2)]
        sg = [sb(f"sg{i}", [128, 128]) for i in range(4)]; T_sg = [T(f"sg{i}") for i in range(4)]
        ot = mrg; T_ot = T_mrg; jk = mrg; T_jk = T_mrg
        gctr = [0]
        for c in range(32):
            tok0 = c * 128
            for q in range(8):
                bufs = qb[q % 2]; Tq = T_q[q % 2]
                cs_ = slice(q * 512, (q + 1) * 512)
                S.dma("sp", bq[:], dsk[:, cs_].partition_broadcast(128), writes=[T_bq])
                S.dma("sp", bufs[0][:], Y1[tok0:tok0 + 128, cs_], reads=[T_scr["Y1"]], writes=[Tq[0]])
                S.dma("sp", bufs[1][:], Y2[tok0:tok0 + 128, cs_], reads=[T_scr["Y2"]], writes=[Tq[1]])
                S.dma("sp", bufs[2][:], XS[tok0:tok0 + 128, cs_], reads=[T_scr["XS"]], writes=[Tq[2]])
                S.dma("sp", bufs[3][:], ZS[tok0:tok0 + 128, cs_], reads=[T_scr["ZS"]], writes=[Tq[3]])
                S.op("dve", lambda e: e.tensor_tensor(out=bufs[2][:], in0=bufs[2][:], in1=bq[:], op=ALU.mult), reads=[Tq[2], T_bq], writes=[Tq[2]])
                S.op("dve", lambda e: e.tensor_tensor(out=bufs[2][:], in0=bufs[2][:], in1=bufs[0][:], op=ALU.add), reads=[Tq[2], Tq[0]], writes=[Tq[2]])
                S.op("dve", lambda e: e.tensor_tensor(out=bufs[2][:], in0=bufs[2][:], in1=bufs[1][:], op=ALU.add), reads=[Tq[2], Tq[1]], writes=[Tq[2]])
                S.op("dve", lambda e: e.tensor_tensor(out=yg[:, cs_], in0=bufs[2][:], in1=bufs[3][:], op=ALU.mult), reads=[Tq[2], Tq[3]], writes=[T_yg])
                S.op("act", lambda e: e.activation(out=bufs[0][:], in_=yg[:, cs_], func=AF.Square, accum_out=ssq8[:, q:q + 1]), reads=[T_yg, Tq[0]], writes=[Tq[0], T_ssq])
            S.op("dve", lambda e: e.tensor_reduce(out=ssq[:, 6:7], in_=ssq8[:], axis=AX.X, op=ALU.add), reads=[T_ssq], writes=[T_ssq])
            rsqrt_col(None, ssq[:, 6:7], 1.0 / DI, T_ssq, None, None)
            for q in range(8):
                cs_ = slice(q * 512, (q + 1) * 512)
                S.dma("sp", bq[:], ssmn[:, cs_].partition_broadcast(128), writes=[T_bq])
                S.op("dve", lambda e: e.scalar_tensor_tensor(out=yg[:, cs_], in0=yg[:, cs_], scalar=ssq[:, 6:7], in1=bq[:], op0=ALU.mult, op1=ALU.mult),
                     reads=[T_yg, T_ssq, T_bq], writes=[T_yg])
            if DBG is not None and c == 0:
                S.dma("sp", DBG[0], yg[:, 0:D], reads=[T_yg])
            for b8 in range(8):
                for q in range(4):
                    fc = b8 * 4 + q
                    S.op("pe", lambda e, fc=fc, q=q: e.transpose(banks[b8][:, q * 128:(q + 1) * 128], yg[:, fc * 128:(fc + 1) * 128], ident[:]),
                         reads=[T_yg, T_c], writes=[T_b[b8]], sig=(q == 3))
                S.op("act" if b8 % 2 else "dve", (lambda e: e.activation(out=ygT[:, b8 * 4:(b8 + 1) * 4, :].rearrange("p a b -> p (a b)"), in_=banks[b8][:], func=AF.Copy)) if b8 % 2 else
                     (lambda e: e.tensor_copy(ygT[:, b8 * 4:(b8 + 1) * 4, :].rearrange("p a b -> p (a b)"), banks[b8][:])), reads=[T_b[b8]], writes=[T_ygT])
            w0 = tok0 - 960
            for j in range(16):
                g_ = gw[j % 2]; Tg = T_gw[j % 2]
                lo = max(w0, 0)
                if lo > w0:
                    S.op("pool", lambda e: e.memset(g_[:, 0:lo - w0], 0.0), writes=[Tg])
                S.dma("sp", g_[:, lo - w0:2048], G[j * 128:(j + 1) * 128, lo:w0 + 2048], reads=[T_scr["G"]], writes=[Tg])
                S.op("dve", lambda e: e.tensor_scalar(out=cf[:, j, :], in0=g_[:, 0:128], scalar1=cfwt[:, j, 0:1], scalar2=cfbt[:, j:j + 1], op0=ALU.mult, op1=ALU.add),
                     reads=[Tg, T_k], writes=[T_cf])
                for k in range(1, 31):
                    S.op("dve", lambda e, k=k: e.scalar_tensor_tensor(out=cf[:, j, :], in0=g_[:, 64 * k:64 * k + 128], scalar=cfwt[:, j, k:k + 1], in1=cf[:, j, :],
                                                                    op0=ALU.mult, op1=ALU.add), reads=[Tg, T_k, T_cf], writes=[T_cf])
            S.op("act", lambda e: e.activation(out=sq[:], in_=cf[:], func=AF.Square), reads=[T_cf], writes=[T_sq])
            for j in range(16):
                S.op("pe", lambda e, j=j: e.matmul(banks[0][:, 0:128], lhsT=ones[:], rhs=cf[:, j, :], start=(j == 0), stop=(j == 15)), reads=[T_c, T_cf], writes=[T_b[0]], sig=(j == 15))
            for j in range(16):
                S.op("pe", lambda e, j=j: e.matmul(banks[1][:, 0:128], lhsT=ones[:], rhs=sq[:, j, :], start=(j == 0), stop=(j == 15)), reads=[T_c, T_sq], writes=[T_b[1]], sig=(j == 15))
            S.op("dve", lambda e: e.tensor_scalar(out=st4[:, 0, :], in0=banks[0][:, 0:128], scalar1=1.0 / D, scalar2=None, op0=ALU.mult), reads=[T_b[0]], writes=[T_st4])
            S.op("dve", lambda e: e.tensor_tensor(out=st4[:, 2, :], in0=st4[:, 0, :], in1=st4[:, 0, :], op=ALU.mult), reads=[T_st4], writes=[T_st4])
            S.op("dve", lambda e: e.scalar_tensor_tensor(out=st4[:, 1, :], in0=banks[1][:, 0:128], scalar=1.0 / D, in1=st4[:, 2, :], op0=ALU.mult, op1=ALU.subtract),
                 reads=[T_b[1], T_st4], writes=[T_st4])
            S.op("dve", lambda e: e.tensor_scalar(out=st4[:, 1, :], in0=st4[:, 1, :], scalar1=EPS, scalar2=None, op0=ALU.add), reads=[T_st4], writes=[T_st4])
            S.op("act", lambda e: e.activation(out=st4[:, 1, :], in_=st4[:, 1, :], func=AF.Ln), reads=[T_st4], writes=[T_st4])
            S.op("act", lambda e: e.activation(out=st4[:, 1, :], in_=st4[:, 1, :], func=AF.Exp, scale=-0.5), reads=[T_st4], writes=[T_st4])
            S.op("dve", lambda e: e.tensor_tensor(out=cf[:], in0=cf[:], in1=st4[:, 0, :].unsqueeze(1).to_broadcast([128, 16, 128]), op=ALU.subtract), reads=[T_cf, T_st4], writes=[T_cf])
            S.op("dve", lambda e: e.tensor_tensor(out=cf[:], in0=cf[:], in1=st4[:, 1, :].unsqueeze(1).to_broadcast([128, 16, 128]), op=ALU.mult), reads=[T_cf, T_st4], writes=[T_cf])
            for j in range(16):
                S.op("dve", lambda e, j=j: e.tensor_scalar(out=cf[:, j, :], in0=cf[:, j, :], scalar1=lngt[:, j:j + 1], scalar2=lnbt[:, j:j + 1], op0=ALU.mult, op1=ALU.add),
                     reads=[T_cf, T_k], writes=[T_cf])
            S.op("act", lambda e: e.activation(out=cfT[:], in_=cf[:], func=AF.Silu), reads=[T_cf], writes=[T_cfT])
            def ev_pa(n, j, bk, Tb, w):
                g_ = gt[gctr[0] % 2]; Tg = T_gt[gctr[0] % 2]; gctr[0] += 1
                S.dma("sp", g_[:], GATE[tok0:tok0 + 128, n * 512:(n + 1) * 512], reads=[T_scr["GATE"]], writes=[Tg])
                S.op("dve", lambda e: e.tensor_tensor(out=mrg[:, n * 512:(n + 1) * 512], in0=bk[:], in1=g_[:], op=ALU.mult), reads=[Tb, Tg], writes=[T_mrg])
            bank0 = gemm(ygT, T_ygT, 32, 128, lambda n: [w_pa_b[:, n * 512:(n + 1) * 512]], 4, "tok", ev_pa, wb, T_wb, KT=16, bank0=0)
            def ev_pb(n, j, bk, Tb, w):
                g_ = gt[gctr[0] % 2]; Tg = T_gt[gctr[0] % 2]; gctr[0] += 1
                S.dma("sp", g_[:], GATE[tok0:tok0 + 128, D + n * 512: D + (n + 1) * 512], reads=[T_scr["GATE"]], writes=[Tg])
                S.op("dve", lambda e: e.tensor_tensor(out=g_[:], in0=bk[:], in1=g_[:], op=ALU.mult), reads=[Tb, Tg], writes=[Tg])
                S.op("dve", lambda e: e.tensor_tensor(out=mrg[:, n * 512:(n + 1) * 512], in0=mrg[:, n * 512:(n + 1) * 512], in1=g_[:], op=ALU.add), reads=[T_mrg, Tg], writes=[T_mrg])
            bank0 = gemm(cfT, T_cfT, 16, 128, lambda n: [w_pb_b[:, n * 512:(n + 1) * 512]], 4, "tok", ev_pb, wb, T_wb, KT=16, bank0=bank0)
            if DBG is not None and c == 0:
                S.dma("sp", DBG[2], mrg[:], reads=[T_mrg])
            for b4 in range(4):
                for q in range(4):
                    kc = b4 * 4 + q
                    S.op("pe", lambda e, kc=kc, q=q: e.transpose(banks[b4][:, q * 128:(q + 1) * 128], mrg[:, kc * 128:(kc + 1) * 128], ident[:]),
                         reads=[T_mrg, T_c], writes=[T_b[b4]], sig=(q == 3))
                S.op("act" if b4 % 2 else "dve", (lambda e: e.activation(out=mT[:, b4 * 4:(b4 + 1) * 4, :].rearrange("p a b -> p (a b)"), in_=banks[b4][:], func=AF.Copy)) if b4 % 2 else
                     (lambda e: e.tensor_copy(mT[:, b4 * 4:(b4 + 1) * 4, :].rearrange("p a b -> p (a b)"), banks[b4][:])), reads=[T_b[b4]], writes=[T_mT])
            S.dma("sp", x1[:], xl[tok0:tok0 + 128, :], writes=[T_x1])
            def ev_o(n, j, bk, Tb, w):
                g_ = gt[gctr[0] % 2]; Tg = T_gt[gctr[0] % 2]; gctr[0] += 1
                S.op("dve", lambda e: e.tensor_tensor(out=g_[:], in0=bk[:], in1=g1bc[:, n * 512:(n + 1) * 512], op=ALU.mult), reads=[Tb, T_c], writes=[Tg])
                S.op("dve", lambda e: e.tensor_tensor(out=x1[:, n * 512:(n + 1) * 512], in0=x1[:, n * 512:(n + 1) * 512], in1=g_[:], op=ALU.add), reads=[T_x1, Tg], writes=[T_x1])
            bank0 = gemm(mT, T_mT, 16, 128, lambda n: [w_o_b[:, n * 512:(n + 1) * 512]], 4, "tok", ev_o, wb, T_wb, KT=16, bank0=4)
            if DBG is not None and c == 0:
                S.dma("sp", DBG[3], x1[:], reads=[T_x1])
            x_to_hT(x1, T_x1, A2, S2, hxT, T_hxT, 0, jk, T_jk, ssq[:, 7:8], T_ssq)
            for n in range(11):
                def ev_g(n_, j, bk, Tb, w):
                    s_ = sg[j]; Ts = T_sg[j]
                    S.op("act", lambda e: e.activation(out=s_[:], in_=bk[:, 0:128], func=AF.Silu), reads=[Tb], writes=[Ts])
                gemm(hxT, T_hxT, 16, 128, lambda n_: [w_g_b[:, n * 512:(n + 1) * 512]], 1, "feat", ev_g, wb, T_wb, KT=16, bank0=0)
                def ev_u(n_, j, bk, Tb, w):
                    s_ = sg[j]; Ts = T_sg[j]
                    S.op("dve", lambda e: e.tensor_tensor(out=actT[:, n * 4 + j, :], in0=bk[:, 0:128], in1=s_[:], op=ALU.mult), reads=[Tb, Ts], writes=[T_actT])
                gemm(hxT, T_hxT, 16, 128, lambda n_: [w_u_b[:, n * 512:(n + 1) * 512]], 1, "feat", ev_u, wb, T_wb, KT=16, bank0=4)
            def ev_d(n, j, bk, Tb, w):
                g_ = gt[gctr[0] % 2]; Tg = T_gt[gctr[0] % 2]; gctr[0] += 1
                S.op("dve", lambda e: e.tensor_tensor(out=g_[:], in0=bk[:], in1=g2bc[:, n * 512:(n + 1) * 512], op=ALU.mult), reads=[Tb, T_c], writes=[Tg])
                S.op("dve", lambda e: e.tensor_tensor(out=x1[:, n * 512:(n + 1) * 512], in0=x1[:, n * 512:(n + 1) * 512], in1=g_[:], op=ALU.add), reads=[T_x1, Tg], writes=[T_x1])
            gemm(actT, T_actT, 44, 128, lambda n: [w_d_b[:, n * 512:(n + 1) * 512]], 4, "tok", ev_d, wb, T_wb, KT=16, bank0=0)
            if DBG is not None and c == 0:
                S.dma("sp", DBG[4], x1[:], reads=[T_x1])
            S.op("act", lambda e: e.activation(out=jk[:], in_=x1[:], func=AF.Square, accum_out=ssq[:, 7:8]), reads=[T_x1], writes=[T_jk, T_ssq])
            rsqrt_col(None, ssq[:, 7:8], 1.0 / D, T_ssq, None, None)
            S.op("dve", lambda e: e.scalar_tensor_tensor(out=ot[:], in0=x1[:], scalar=ssq[:, 7:8], in1=nfbc[:], op0=ALU.mult, op1=ALU.mult),
                 reads=[T_x1, T_ssq, T_k], writes=[T_ot])
            S.dma("sp", out[tok0:tok0 + 128, :], ot[:], reads=[T_ot], writes=[T_scr["Y1"]] if False else [])
        barrier()
    P.close()
    return nc, S


def _colT(v, n):
    return np.ascontiguousarray(np.asarray(v, np.float32).reshape(n, 128).T)


def kernel(x, c, ctx, c_ctx, w_mod, b_mod, norm_mix, w_in, ssm_conv_w, ssm_conv_b, dt_bias, a_log, d_skip, ssm_norm,
           cf_conv_w, cf_conv_b, cf_ln_g, cf_ln_b, w_proj_a, w_proj_b, w_out, norm_ffn, w_ffn_gate, w_ffn_up,
           w_ffn_down, norm_final):
    in_maps = make_in_maps(x, c, ctx, c_ctx, w_mod, b_mod, norm_mix, w_in, ssm_conv_w, ssm_conv_b, dt_bias, a_log, d_skip, ssm_norm,
                           cf_conv_w, cf_conv_b, cf_ln_g, cf_ln_b, w_proj_a, w_proj_b, w_out, norm_ffn, w_ffn_gate, w_ffn_up,
                           w_ffn_down, norm_final)
    nc, _ = build_nc()
    res = run_bass_kernel_spmd(nc, in_maps, core_ids=list(range(8)))
    outp = np.empty((4, L, D), np.float32)
    for core in range(8):
        b, hf = core // 2, core % 2
        o = np.asarray(res.results[core]["out"])
        if hf == 0:
            outp[b, :LOWN] = o
        else:
            outp[b, LOWN:] = o[::-1]
    return outp


def make_in_maps(x, c, ctx, c_ctx, w_mod, b_mod, norm_mix, w_in, ssm_conv_w, ssm_conv_b, dt_bias, a_log, d_skip, ssm_norm,
                 cf_conv_w, cf_conv_b, cf_ln_g, cf_ln_b, w_proj_a, w_proj_b, w_out, norm_ffn, w_ffn_gate, w_ffn_up,
                 w_ffn_down, norm_final):
    f = lambda a: np.ascontiguousarray(np.asarray(a, np.float32))
    x = f(x); ctx = f(ctx)
    shared = {
        "w_mod": f(w_mod[0]), "bmodT": _colT(b_mod[0], 96), "nmixT": _colT(norm_mix[0], 16), "nffnT": _colT(norm_ffn[0], 16),
        "w_in": f(w_in[0]), "cb": _colT(ssm_conv_b[0], 48),
        "dsk": f(np.repeat(np.asarray(d_skip[0], np.float32), 64)[None, :]), "ssmn": f(np.asarray(ssm_norm[0])[None, :]),
        "cfb": _colT(cf_conv_b[0], 16), "lng": _colT(cf_ln_g[0], 16), "lnb": _colT(cf_ln_b[0], 16),
        "w_pa": f(w_proj_a[0]), "w_pb": f(w_proj_b[0]), "w_o": f(w_out[0]),
        "w_g": f(w_ffn_gate[0]), "w_u": f(w_ffn_up[0]), "w_d": f(w_ffn_down[0]), "nfin": f(np.asarray(norm_final)[None, :]),
    }
    in_maps = []
    for core in range(8):
        b, hf = core // 2, core % 2
        dA, dB = (0, 1) if hf == 0 else (1, 0)
        m = dict(shared)
        xb = x[b]; cb_ = ctx[b]
        cwk = np.asarray(ssm_conv_w[0], np.float32); cfk = np.asarray(cf_conv_w[0], np.float32)
        if hf == 1:
            xb = xb[::-1]; cb_ = cb_[::-1]; cwk = cwk[::-1]; cfk = cfk[::-1]
        m["xl"] = f(xb); m["ctxl"] = f(cb_)
        m["c2T"] = f(np.stack([_colT(c[b], 16), _colT(c_ctx, 16)], axis=1))
        wi = np.asarray(w_in[0])
        m["w_dt"] = f(np.concatenate([wi[:, OFF_XBC + dA * 64: OFF_XBC + dA * 64 + 64], wi[:, OFF_XBC + dB * 64: OFF_XBC + dB * 64 + 64]], axis=1))
        m["dtb"] = f(np.concatenate([dt_bias[0][dA], dt_bias[0][dB]])[None, :])
        m["alog"] = f(np.concatenate([a_log[0][dA], a_log[0][dB]])[None, :])
        m["cw"] = f(cwk.T.reshape(48, 128, 5).transpose(1, 0, 2))
        m["cfw"] = f(cfk.T.reshape(16, 128, 31).transpose(1, 0, 2))
        in_maps.append(m)
    return in_maps
```
